# Optimizing a Trainium2 kernel written in Bass

```python
import math
import jax, jax.numpy as jnp
from jax import lax
import numpy as np

D_MODEL = 1024
BATCH = 16
SEQ = 2048
DEPTH = 1

HEAD_DIM = 64
D_MIX = D_MODEL
A_GROUPS = 4
A_WIDTH = A_GROUPS * HEAD_DIM
B_HEADS = 12
B_WIDTH = B_HEADS * HEAD_DIM
CHUNK = 128
DILATED_CONFIGS = ((128, 1), (512, 4), (2048, 16))
NUM_BUCKETS = 32
MAX_DISTANCE = 2048
D_FF = 2816
CONV_WIDTH = 3
IN_COLS = 2 * A_WIDTH + 3 * B_WIDTH
NORM_EPS = 1e-6
NEG_INF = -1e30

kernel_name = "hybrid_gmlp_dilated_attn_convffn"


def rms_norm(x, g):
    xf = x.astype(jnp.float32)
    y = xf * lax.rsqrt(jnp.mean(xf * xf, axis=-1, keepdims=True) + NORM_EPS)
    return (y * g.astype(jnp.float32)).astype(x.dtype)


def t5_bucket(dist):
    max_exact = NUM_BUCKETS // 2
    d = jnp.maximum(dist, 1).astype(jnp.float32)
    large = max_exact + (jnp.log(d / max_exact) / math.log(MAX_DISTANCE / max_exact)
                         * (NUM_BUCKETS - max_exact))
    large = jnp.minimum(large.astype(jnp.int32), NUM_BUCKETS - 1)
    return jnp.where(dist < max_exact, dist, large)


def spatial_gating(u, v, ln_g, ln_b, w_s, b_s):
    B, T, G, hd = u.shape
    u = jax.nn.gelu(u)
    vf = jax.nn.gelu(v).astype(jnp.float32)
    mu = jnp.mean(vf, axis=-1, keepdims=True)
    var = jnp.mean(jnp.square(vf - mu), axis=-1, keepdims=True)
    vn = (vf - mu) * lax.rsqrt(var + NORM_EPS) * ln_g.astype(jnp.float32) + ln_b.astype(jnp.float32)
    vc = vn.reshape(B, T // CHUNK, CHUNK, G, hd)
    tril = jnp.tril(jnp.ones((CHUNK, CHUNK), jnp.float32))
    w = w_s.astype(jnp.float32) * tril[None]
    z = jnp.einsum('gts,bcsgd->bctgd', w, vc) + b_s.astype(jnp.float32).T[None, None, :, :, None]
    return u * z.reshape(B, T, G, hd).astype(u.dtype)


def dilated_segment(q, k, v, rel_bias, window, dil):
    B, T, H, D = q.shape
    nw = window // dil
    seg = nw * dil
    nb = -(-T // seg)
    Tp = nb * seg

    def blocks(a):
        a = jnp.pad(a, ((0, 0), (0, Tp - T), (0, 0), (0, 0)))
        return a.reshape(B, nb, nw, dil, H, D)

    def with_prev(a):
        prev = jnp.pad(a, ((0, 0), (1, 0), (0, 0), (0, 0), (0, 0), (0, 0)))[:, :-1]
        return jnp.concatenate([prev, a], axis=2)

    qb = blocks(q)
    kc = with_prev(blocks(k))
    vc = with_prev(blocks(v))

    i = jnp.arange(nw)[:, None]
    j = jnp.arange(2 * nw)[None, :]
    rel = nw + i - j
    band = (rel >= 0) & (rel <= nw)
    key_ok = (jnp.arange(nb)[:, None] * nw + jnp.arange(2 * nw)[None, :] - nw) >= 0
    mask = band[None] & key_ok[:, None, :]
    bias = rel_bias.astype(jnp.float32)[t5_bucket(jnp.maximum(rel, 0) * dil)]
    bias = bias.transpose(2, 0, 1)

    scale = 1.0 / math.sqrt(D)
    logits = jnp.einsum('bnirhd,bnjrhd->bnrhij', qb, kc) * scale + bias[None, None, None]
    logits = jnp.where(mask[None, :, None, None], logits, NEG_INF)
    lse = jax.nn.logsumexp(logits, axis=-1)
    p = jnp.exp(logits - lse[..., None])
    o = jnp.einsum('bnrhij,bnjrhd->bnirhd', p, vc)
    o = o.reshape(B, Tp, H, D)[:, :T]
    lse = lse.transpose(0, 1, 4, 2, 3).reshape(B, Tp, H)[:, :T]
    return o, lse


def dilated_attention(q, k, v, rel_bias):
    outs, lses = [], []
    for window, dil in DILATED_CONFIGS:
        o, lse = dilated_segment(q, k, v, rel_bias, window, dil)
        outs.append(o)
        lses.append(lse)
    o = jnp.stack(outs, axis=0)
    w = jax.nn.softmax(jnp.stack(lses, axis=0), axis=0)
    return jnp.sum(w[..., None] * o, axis=0)


def causal_dwconv(h, w, b):
    T = h.shape[1]
    hp = jnp.pad(h, ((0, 0), (CONV_WIDTH - 1, 0), (0, 0)))
    out = b
    for kk in range(CONV_WIDTH):
        out = out + hp[:, kk:kk + T] * w[kk]
    return out


def setup_inputs(seed: int = 0) -> dict:
    key = jax.random.key(seed)
    ks = jax.random.split(key, 20)
    f32 = jnp.float32

    def nrm(k, shape, s):
        return jax.random.normal(k, shape, f32) * s

    L = DEPTH
    return {
        "x": jax.random.normal(ks[0], (BATCH, SEQ, D_MODEL), f32),
        "norm_mix_pre": 1.0 + nrm(ks[1], (L, D_MODEL), 0.01),
        "norm_mix_post": 1.0 + nrm(ks[2], (L, D_MODEL), 0.01),
        "norm_ffn_pre": 1.0 + nrm(ks[3], (L, D_MODEL), 0.01),
        "norm_ffn_post": 1.0 + nrm(ks[4], (L, D_MODEL), 0.01),
        "w_in": nrm(ks[5], (L, D_MODEL, IN_COLS), D_MODEL ** -0.5),
        "ln_v_gain": 1.0 + nrm(ks[6], (L, A_GROUPS, HEAD_DIM), 0.01),
        "ln_v_bias": nrm(ks[7], (L, A_GROUPS, HEAD_DIM), 0.01),
        "spatial_w": nrm(ks[8], (L, A_GROUPS, CHUNK, CHUNK), CHUNK ** -0.5),
        "spatial_b": 1.0 + nrm(ks[9], (L, A_GROUPS, CHUNK), 0.01),
        "rel_bias": nrm(ks[10], (NUM_BUCKETS, B_HEADS), 0.5),
        "w_out": nrm(ks[11], (L, D_MIX, D_MODEL), D_MIX ** -0.5),
        "w_gate": nrm(ks[12], (L, D_MODEL, D_FF), D_MODEL ** -0.5),
        "w_up": nrm(ks[13], (L, D_MODEL, D_FF), D_MODEL ** -0.5),
        "conv_w": nrm(ks[14], (L, CONV_WIDTH, D_FF), CONV_WIDTH ** -0.5),
        "conv_b": nrm(ks[15], (L, D_FF), 0.01),
        "w_down": nrm(ks[16], (L, D_FF, D_MODEL), D_FF ** -0.5),
    }


def reference(x, norm_mix_pre, norm_mix_post, norm_ffn_pre, norm_ffn_post, w_in,
              ln_v_gain, ln_v_bias, spatial_w, spatial_b, rel_bias, w_out,
              w_gate, w_up, conv_w, conv_b, w_down):
    B, T, _ = x.shape
    for l in range(DEPTH):
        h = rms_norm(x, norm_mix_pre[l])
        proj = h @ w_in[l]
        o0 = 0
        ua = proj[..., o0:o0 + A_WIDTH].reshape(B, T, A_GROUPS, HEAD_DIM); o0 += A_WIDTH
        va = proj[..., o0:o0 + A_WIDTH].reshape(B, T, A_GROUPS, HEAD_DIM); o0 += A_WIDTH
        qb = proj[..., o0:o0 + B_WIDTH].reshape(B, T, B_HEADS, HEAD_DIM); o0 += B_WIDTH
        kb = proj[..., o0:o0 + B_WIDTH].reshape(B, T, B_HEADS, HEAD_DIM); o0 += B_WIDTH
        vb = proj[..., o0:o0 + B_WIDTH].reshape(B, T, B_HEADS, HEAD_DIM)

        a_out = spatial_gating(ua, va, ln_v_gain[l], ln_v_bias[l], spatial_w[l], spatial_b[l])
        b_out = dilated_attention(qb.astype(jnp.float32), kb.astype(jnp.float32),
                                  vb.astype(jnp.float32), rel_bias)
        mix = jnp.concatenate([a_out.reshape(B, T, A_WIDTH),
                               b_out.reshape(B, T, B_WIDTH).astype(x.dtype)], axis=-1)
        x = x + rms_norm(mix @ w_out[l], norm_mix_post[l])

        h = rms_norm(x, norm_ffn_pre[l])
        g = jax.nn.gelu(causal_dwconv(h @ w_gate[l], conv_w[l], conv_b[l]))
        y = (g * (h @ w_up[l])) @ w_down[l]
        x = x + rms_norm(y, norm_ffn_post[l])
    return x
```

```python
import math
from contextlib import ExitStack
import numpy as np
import concourse.bass as bass
import concourse.mybir as mybir
from concourse.bass_utils import run_bass_kernel_spmd

F32 = mybir.dt.float32
BF16 = mybir.dt.bfloat16
AF = mybir.ActivationFunctionType
ALU = mybir.AluOpType
AX = mybir.AxisListType

ENGS = ("pe", "act", "dve", "pool", "sp")


class _Rec:
    def __getattr__(self, name):
        return lambda *a, **k: (name, a, k)


_REC = _Rec()


class Sched:
    def __init__(self, nc):
        self.nc = nc
        self.ops = {e: [] for e in ENGS}
        self.lastw = {}
        self.readers = {}
        self.dma_cnt = {}
        self.pending = {e: set() for e in ENGS}

    def _collect(self, eng, reads, writes):
        deps = set(self.pending[eng])
        self.pending[eng] = set()
        for k in reads:
            w = self.lastw.get(k)
            if w is not None:
                deps.add(w)
        for k in writes:
            w = self.lastw.get(k)
            if w is not None:
                deps.add(w)
            for r in self.readers.get(k, ()):
                deps.add(r)
        if eng == "pe":
            deps = {d for d in deps if not (d[0] == "E" and d[1] == "pe")}
        return deps

    def _update(self, ev, reads, writes):
        ws = set(writes)
        for k in reads:
            if k not in ws:
                self.readers.setdefault(k, []).append(ev)
        for k in writes:
            self.lastw[k] = ev
            self.readers[k] = []

    def op(self, eng, fn, reads=(), writes=()):
        name, a, k = fn(_REC)
        fn = (lambda h, name=name, a=a, k=k: getattr(h, name)(*a, **k))
        psr = [k for k in reads if isinstance(k, tuple) and k[0] == "ps"]
        if psr:
            reads = [k for k in reads if k not in psr]
            writes = list(writes) + psr
        deps = self._collect(eng, reads, writes)
        idx = len(self.ops[eng])
        ev = ("E", eng, idx)
        deps.discard(ev)
        self.ops[eng].append(dict(fn=fn, deps=deps, sig=False, dma=None))
        self._update(ev, reads, writes)
        return ev

    def dma(self, q, out, in_, reads, writes, semkey, **kw):
        deps = self._collect(q, reads, writes)
        val = self.dma_cnt.get(semkey, 0) + 16
        self.dma_cnt[semkey] = val
        ev = ("D", semkey, val)
        self.ops[q].append(dict(fn=lambda e: e.dma_start(out=out, in_=in_, **kw),
                                deps=deps, sig=False, dma=semkey))
        self._update(ev, reads, writes)
        return ev

    def barrier(self):
        evs = set()
        for e in ENGS:
            if self.ops[e]:
                for i in range(len(self.ops[e]) - 1, -1, -1):
                    if self.ops[e][i]["dma"] is None:
                        evs.add(("E", e, i))
                        break
        for k, v in self.dma_cnt.items():
            evs.add(("D", k, v))
        for e in ENGS:
            self.pending[e] |= evs
        self.lastw = {}
        self.readers = {}

    def finish(self):
        evs = {("D", k, v) for k, v in self.dma_cnt.items()}
        self.pending["sp"] |= evs
        self.ops["sp"].append(dict(fn=None, deps=self._collect("sp", (), ()), sig=False, dma=None))

    def emit(self):
        nc = self.nc
        for e in ENGS:
            for o in self.ops[e]:
                for d in o["deps"]:
                    if d[0] == "E":
                        self.ops[d[1]][d[2]]["sig"] = True
        for e in ENGS:
            c = 0
            for o in self.ops[e]:
                if o["sig"]:
                    c += 1
                o["sigval"] = c
        with ExitStack() as st:
            esem = {e: st.enter_context(nc.semaphore("e_" + e)) for e in ENGS}
            dsem = {}
            for i, k in enumerate(self.dma_cnt):
                dsem[k] = st.enter_context(nc.semaphore("d%d" % i))
            block = st.enter_context(nc.Block())

            def run(ename, h):
                waited = {}
                for o in self.ops[ename]:
                    need = {}
                    for d in o["deps"]:
                        if d[0] == "E":
                            key, sem, val = ("E", d[1]), esem[d[1]], self.ops[d[1]][d[2]]["sigval"]
                        else:
                            key, sem, val = ("D", d[1]), dsem[d[1]], d[2]
                        if val > need.get(key, (None, 0))[1]:
                            need[key] = (sem, val)
                    for key, (sem, val) in need.items():
                        if val > waited.get(key, 0):
                            h.wait_ge(sem, val)
                            waited[key] = val
                    if o["fn"] is None:
                        continue
                    ins = o["fn"](h)
                    if o["dma"] is not None:
                        ins.then_inc(dsem[o["dma"]], 16)
                    elif o["sig"]:
                        ins.then_inc(esem[ename], 1)

            @block.tensor
            def _(h):
                run("pe", h)

            @block.scalar
            def _(h):
                run("act", h)

            @block.vector
            def _(h):
                run("dve", h)

            @block.gpsimd
            def _(h):
                run("pool", h)

            @block.sync
            def _(h):
                run("sp", h)
        return len(self.dma_cnt)


D = 1024
T = 2048
NSEQ = 2
NTOK = NSEQ * T
INC = 2816
DFF = 2816
NFC = DFF // 128
EPS = 1e-6
MASKV = -30000.0
ARENA_ELEMS = 104960


def _t5_bucket_np(dist):
    d = np.maximum(dist, 1).astype(np.float32)
    large = np.float32(16) + (np.log(d / np.float32(16)) / np.float32(math.log(128.0))
                              * np.float32(16))
    large = np.minimum(large.astype(np.int32), 31)
    return np.where(dist < 16, dist, large)


def _host_consts():
    ident = np.eye(128, dtype=np.float32)
    J = np.ascontiguousarray(ident[::-1])
    tril = np.tril(np.ones((128, 128), np.float32))
    oh = np.zeros((33, 6 * 255), np.float32)
    dils = (1, 4, 16)
    for cp in range(6):
        c, pc = cp // 2, cp % 2
        for n in range(255):
            m = 254 - n
            if pc == 0:
                rel, ok = m - 127, m >= 127
            else:
                rel, ok = m + 1, m <= 127
            if ok:
                b = int(_t5_bucket_np(np.array([rel * dils[c]]))[0])
            else:
                b = 32
            oh[b, cp * 255 + n] = 1.0
    return ident, J, tril, oh


class Arena:
    def __init__(self, ap, total):
        self.ap, self.total, self.off = ap, total, 0

    def mark(self):
        return self.off

    def release(self, m):
        self.off = m

    def alloc(self, shape, dt, parts=128):
        n = 1
        for s in shape:
            n *= s
        ne = n * (2 if dt == F32 else 1)
        ne = (ne + 31) // 32 * 32
        start = self.off
        self.off += ne
        assert self.off <= self.total, ("arena overflow", self.off, self.total)
        a = self.ap[0:parts, start:start + n * (2 if dt == F32 else 1)]
        if dt == F32:
            a = a.bitcast(F32)
        if len(shape) == 2:
            a = a.rearrange("p (a b) -> p a b", a=shape[0], b=shape[1])
        elif len(shape) == 3:
            a = a.rearrange("p (a b c) -> p a b c", a=shape[0], b=shape[1], c=shape[2])
        return a


def build_nc(stop_after_mixer=False, stop_stage=99, dbg=False, dbgf=False):
    nc = bass.Bass("TRN2", target_bir_lowering=False)

    def din(name, shape):
        return nc.dram_tensor(name, list(shape), F32, kind="ExternalInput").ap()

    x = din("x", [NTOK, D])
    g_in = [din("g%d" % i, [1, D]) for i in range(4)]
    w_in = din("w_in", [D, INC])
    ln_g = din("ln_g", [1, 256])
    ln_b = din("ln_b", [1, 256])
    sp_w = din("sp_w", [4, 128, 128])
    sp_b = din("sp_b", [4, 128])
    rel_b = din("rel_b", [32, 12])
    w_out = din("w_out", [D, D])
    w_gate = din("w_gate", [D, DFF])
    w_up = din("w_up", [D, DFF])
    conv_w = din("conv_w", [3, DFF])
    conv_b = din("conv_b", [1, DFF])
    w_down = din("w_down", [DFF, D])
    c_ident = din("c_ident", [128, 128])
    c_J = din("c_J", [128, 128])
    c_tril = din("c_tril", [128, 128])
    c_oh = din("c_oh", [33, 1530])
    out = nc.dram_tensor("out", [NTOK, D], F32, kind="ExternalOutput").ap()
    vs = nc.dram_tensor("vs", [6, T, 130], BF16, kind="Internal").ap()
    gs_t = nc.dram_tensor("gs", [12, 1530], F32, kind="Internal")
    gs = gs_t.ap()

    S = Sched(nc)

    def dump(name, src, shape, reads, dt=BF16):
        t = nc.dram_tensor(name, list(shape), dt, kind="ExternalOutput").ap()
        S.dma("sp", t, src, reads, [], ("dbg", name))

    with ExitStack() as st:
        arena_t = st.enter_context(nc.sbuf_tensor("arena", [128, ARENA_ELEMS], BF16))
        pA = st.enter_context(nc.psum_tensor("pA", [128, 2048], F32))
        pB = st.enter_context(nc.psum_tensor("pB", [128, 2048], F32))
        AR = Arena(arena_t, ARENA_ELEMS)

        def bank(i, parts=slice(0, 128)):
            p = pA if i < 4 else pB
            return p[parts, (i % 4) * 512:(i % 4 + 1) * 512]

        def bankT(i):
            return bank(i).bitcast(BF16)

        PS = lambda i: ("ps", i)

        identb = AR.alloc([128], BF16)
        Jb = AR.alloc([128], BF16)
        identf = AR.alloc([128], F32)
        onesf = AR.alloc([64], F32)
        WmT = AR.alloc([4, 128], BF16)
        bs = AR.alloc([4], F32)
        cwt = AR.alloc([4, NFC], F32)
        lng = AR.alloc([256], F32)
        lnb = AR.alloc([256], F32)
        gbc = [AR.alloc([D], F32) for _ in range(2)]
        ss = AR.alloc([16], F32)
        sq = AR.alloc([16], F32)
        rs = AR.alloc([16], F32)
        junk = AR.alloc([D], BF16)
        wr = [AR.alloc([8, 256], BF16) for _ in range(4)]
        xt = [AR.alloc([D], F32) for _ in range(2)]
        hb = [AR.alloc([D], BF16) for _ in range(2)]

        cnt = {"xt": 0, "hb": 0, "bank": 0}

        w_in_v = w_in.rearrange("(c p) n -> p c n", p=128)
        w_out_v = w_out.rearrange("(c p) n -> p c n", p=128)
        w_gate_v = w_gate.rearrange("(c p) n -> p c n", p=128)
        w_up_v = w_up.rearrange("(c p) n -> p c n", p=128)
        WQ = []
        for s in range(NSEQ):
            for u in range(11):
                WQ.append(w_in_v[:, :, u * 256:(u + 1) * 256])
            for u in range(4):
                WQ.append(w_out_v[:, :, u * 256:(u + 1) * 256])
        if not stop_after_mixer:
            for tt in range(4):
                for j in range(11):
                    WQ.append(w_gate_v[:, :, j * 256:(j + 1) * 256])
                    WQ.append(w_up_v[:, :, j * 256:(j + 1) * 256])
        wstate = {"issued": 0, "used": 0, "done": 0}

        def w_issue_upto(n):
            while wstate["issued"] < min(n, len(WQ)):
                i = wstate["issued"]
                sl = i % 4
                S.dma("pool", wr[sl], WQ[i], [], [("wr", sl)], ("wr", sl))
                wstate["issued"] += 1

        def wnext():
            i = wstate["used"]
            wstate["used"] += 1
            w_issue_upto(wstate["done"] + 4)
            assert i < wstate["issued"] or i >= len(WQ)
            return i % 4

        def wdone(n=1):
            wstate["done"] += n
            w_issue_upto(wstate["done"] + 4)

        S.dma("pool", identb, c_ident[:, :], [], ["identb"], "c0")
        S.dma("pool", Jb, c_J[:, :], [], ["Jb"], "c1")
        S.dma("sp", identf, c_ident[:, :], [], ["identf"], "c2")
        S.dma("sp", lng, ln_g[0:1, :].partition_broadcast(128), [], ["lng"], "c3")
        S.dma("sp", lnb, ln_b[0:1, :].partition_broadcast(128), [], ["lnb"], "c4")
        S.op("dve", lambda e: e.memset(onesf, 1.0), [], ["onesf"])
        w_issue_upto(4)

        def load_gains(i0):
            for k in range(2):
                S.dma("sp", gbc[k], g_in[i0 + k][0:1, :].partition_broadcast(128), [],
                      [("gbc", k)], ("gbc", k))

        load_gains(0)

        m_setup = AR.mark()
        biasT = AR.alloc([6, 12, 128], BF16)
        big = AR.alloc([8, T], BF16)
        aT = AR.alloc([2, T], BF16)
        m_ma = AR.mark()
        ohb = AR.alloc([1530], BF16)
        rbx = AR.alloc([12], BF16)
        gsb = AR.alloc([1530], F32)
        wsb = AR.alloc([4, 128], F32)
        trl = AR.alloc([128], F32)
        wmb = AR.alloc([4, 128], BF16)
        cwr = AR.alloc([4, 128], F32)
        sb4 = AR.alloc([128], F32)

        S.dma("pool", ohb[0:33, :], c_oh[:, :], [], ["ohb"], "c5")
        S.dma("pool", rbx[0:32, :], rel_b[:, :], [], ["rbx0"], "c6")
        S.op("dve", lambda e: e.memset(rbx[32:33, :], MASKV), [], ["rbx1"])
        for k in range(3):
            S.op("pe", lambda e, k=k: e.matmul(bank(0)[0:12, 0:510], lhsT=rbx[0:33, 0:12],
                                               rhs=ohb[0:33, k * 510:(k + 1) * 510],
                                               start=True, stop=True),
                 ["rbx0", "rbx1", "ohb"], [PS(0)])
            S.op("act", lambda e, k=k: e.activation(out=gsb[0:12, k * 510:(k + 1) * 510],
                                                    in_=bank(0)[0:12, 0:510], func=AF.Copy),
                 [PS(0)], ["gsb"])
        S.dma("sp", gs[:, :], gsb[0:12, :], ["gsb"], ["gs"], "c7")
        for cp in range(6):
            src = bass.AP(tensor=gs_t, offset=cp * 255, ap=[[1, 128], [1530, 12], [1, 128]])
            S.dma("pool", biasT[:, cp, :, :], src, ["gs"], ["biasT"], "c8")
        S.dma("sp", wsb, sp_w.rearrange("g t s -> t g s"), [], ["wsb"], "c9")
        S.dma("sp", trl, c_tril[:, :], [], ["trl"], "c10")
        for g in range(4):
            S.op("dve", lambda e, g=g: e.tensor_tensor(out=wmb[:, g, :], in0=wsb[:, g, :], in1=trl,
                                                       op=ALU.mult), ["wsb", "trl"], [("wmb", g)])
            S.op("pe", lambda e, g=g: e.transpose(out=bankT(7)[:, g * 128:(g + 1) * 128],
                                                  in_=wmb[:, g, :], identity=identb),
                 [("wmb", g), "identb"], [PS(7)])
        S.op("act", lambda e: e.activation(out=WmT.rearrange("p a b -> p (a b)"),
                                           in_=bankT(7)[:, 0:512], func=AF.Copy), [PS(7)], ["WmT"])
        for k in range(3):
            S.dma("sp", cwr[0:NFC, k, :], conv_w[k].rearrange("(j p) -> j p", p=128), [], ["cwr"], "c11")
        S.dma("sp", cwr[0:NFC, 3, :], conv_b[0].rearrange("(j p) -> j p", p=128), [], ["cwr"], "c11")
        S.dma("sp", sb4[0:4, :], sp_b[:, :], [], ["sb4"], "c12")
        for k in range(4):
            S.op("pe", lambda e, k=k: e.matmul(bank(1)[:, k * NFC:(k + 1) * NFC], lhsT=cwr[0:NFC, k, :],
                                               rhs=identf[0:NFC, 0:NFC], start=True, stop=True),
                 ["cwr", "identf"], [PS(1)])
        S.op("pe", lambda e: e.matmul(bank(1)[:, 128:132], lhsT=sb4[0:4, :], rhs=identf[0:4, 0:4],
                                      start=True, stop=True), ["sb4", "identf"], [PS(1)])
        S.op("act", lambda e: e.activation(out=cwt.rearrange("p a b -> p (a b)"),
                                           in_=bank(1)[:, 0:4 * NFC], func=AF.Copy), [PS(1)], ["cwt"])
        S.op("act", lambda e: e.activation(out=bs, in_=bank(1)[:, 128:132], func=AF.Copy), [PS(1)], ["bs"])
        S.barrier()
        AR.release(m_ma)

        def norm_transpose(src_rows, gk, sub, dst, dst_keys):
            xs = cnt["xt"] % 2
            cnt["xt"] += 1
            S.dma("sp", xt[xs], src_rows, [], [("xt", xs)], ("xt", xs))
            col = sub % 16
            S.op("act", lambda e: e.activation(out=junk, in_=xt[xs], func=AF.Square, scale=1.0 / 32,
                                               accum_out=ss[:, col:col + 1]), [("xt", xs)], [("ss", col)])
            S.op("act", lambda e: e.activation(out=sq[:, col:col + 1], in_=ss[:, col:col + 1], func=AF.Sqrt,
                                               bias=EPS, scale=1.0), [("ss", col)], [("sq", col)])
            S.op("dve", lambda e: e.reciprocal(out=rs[:, col:col + 1], in_=sq[:, col:col + 1]),
                 [("sq", col)], [("rs", col)])
            hs = cnt["hb"] % 2
            cnt["hb"] += 1
            S.op("dve", lambda e: e.scalar_tensor_tensor(out=hb[hs], in0=xt[xs], scalar=rs[:, col:col + 1],
                                                         in1=gbc[gk], op0=ALU.mult, op1=ALU.mult),
                 [("xt", xs), ("rs", col), ("gbc", gk)], [("hb", hs)])
            tb = 6 + (sub % 2)
            for c in range(8):
                S.op("pe", lambda e, c=c: e.transpose(out=bankT(tb)[:, c * 128:(c + 1) * 128],
                                                      in_=hb[hs][:, c * 128:(c + 1) * 128], identity=identb),
                     [("hb", hs), "identb"], [PS(tb)])
            src = bankT(tb).rearrange("p (c t) -> p c t", c=8)
            if sub % 2 == 0:
                S.op("act", lambda e: e.activation(out=dst, in_=src, func=AF.Copy), [PS(tb)], dst_keys)
            else:
                S.op("dve", lambda e: e.tensor_copy(out=dst, in_=src), [PS(tb)], dst_keys)

        def post_norm_residual(yps, ybanks, gk, res_rows, out_rows, tmpb, x1b, idx):
            col = idx % 16
            rk = [PS(b) for b in ybanks]
            xs = cnt["xt"] % 2
            cnt["xt"] += 1
            S.dma("sp", xt[xs], res_rows, [], [("xt", xs)], ("xt", xs))
            S.op("act", lambda e: e.activation(out=junk, in_=yps, func=AF.Square, scale=1.0 / 32,
                                               accum_out=ss[:, col:col + 1]), rk, [("ss", col)])
            S.op("act", lambda e: e.activation(out=sq[:, col:col + 1], in_=ss[:, col:col + 1], func=AF.Sqrt,
                                               bias=EPS, scale=1.0), [("ss", col)], [("sq", col)])
            S.op("dve", lambda e: e.reciprocal(out=rs[:, col:col + 1], in_=sq[:, col:col + 1]),
                 [("sq", col)], [("rs", col)])
            ts = idx % 2
            S.op("dve", lambda e: e.scalar_tensor_tensor(out=tmpb[ts], in0=yps, scalar=rs[:, col:col + 1],
                                                         in1=gbc[gk], op0=ALU.mult, op1=ALU.mult),
                 rk + [("rs", col), ("gbc", gk)], [("tmp", ts)])
            S.op("dve", lambda e: e.tensor_tensor(out=x1b[ts], in0=tmpb[ts], in1=xt[xs], op=ALU.add),
                 [("tmp", ts), ("xt", xs)], [("x1b", ts)])
            S.dma("sp", out_rows, x1b[ts], [("x1b", ts)], [("out", idx)], ("x1b", ts))

        for s in range(NSEQ):
            base = s * T
            m_seq = AR.mark()
            qT = AR.alloc([6, T], BF16)
            kT = AR.alloc([6, T], BF16)
            ug = AR.alloc([16, 256], BF16)
            Vc = [AR.alloc([16, 130], BF16) for _ in range(6)]
            vst = [AR.alloc([2, 2, 65], BF16) for _ in range(2)]
            PT = [AR.alloc([512], BF16) for _ in range(3)]
            vg = AR.alloc([512], F32)
            cen = AR.alloc([512], F32)
            sqb = AR.alloc([512], F32)
            vnb = AR.alloc([512], BF16)
            ab = AR.alloc([512], BF16)
            st1 = AR.alloc([8], F32)
            st2 = AR.alloc([8], F32)
            st3 = AR.alloc([8], F32)
            rden = AR.alloc([512], F32)
            bcs = AR.alloc([512], F32)
            ostg = AR.alloc([T], BF16)

            for i in range(2):
                S.op("dve", lambda e, i=i: e.memset(vst[i], 1.0), [], [("vst", i)])

            for sub in range(16):
                norm_transpose(x[base + sub * 128: base + (sub + 1) * 128, :], 0, sub,
                               big[:, :, sub * 128:(sub + 1) * 128], [("big", c, sub) for c in range(8)])

            if dbg and s == 0:
                dump("d_hT", big.rearrange("p a b -> p (a b)"), [128, 8 * T], [("big", c, i) for c in range(8) for i in range(16)])
                dump("d_bias", biasT.rearrange("p a b c -> p (a b c)"), [128, 6 * 12 * 128], ["biasT"])
                dump("d_WmT", WmT.rearrange("p a b -> p (a b)"), [128, 512], ["WmT"])
                dump("d_bs", bs, [128, 4], ["bs"], F32)
                dump("d_cwt", cwt.rearrange("p a b -> p (a b)"), [128, 4 * NFC], ["cwt"], F32)
            if stop_stage < 2:
                break

            def hkeys(subs):
                return [("big", c, sb_) for c in range(8) for sb_ in subs]

            def gbank():
                b = cnt["bank"] % 6
                cnt["bank"] += 1
                return b

            sl = wnext()
            for sp_ in range(8):
                b = gbank()
                for k in range(2):
                    sub = sp_ * 2 + k
                    for c in range(8):
                        S.op("pe", lambda e, c=c, k=k, sub=sub, b=b: e.matmul(
                            bank(b)[:, k * 256:(k + 1) * 256], lhsT=big[:, c, sub * 128:(sub + 1) * 128],
                            rhs=wr[sl][:, c, :], start=(c == 0), stop=(c == 7)),
                            [("big", c, sub), ("wr", sl)], [PS(b)])
                S.op("act", lambda e, sp_=sp_, b=b: e.activation(
                    out=ug[:, sp_ * 2:sp_ * 2 + 2, :].rearrange("p a b -> p (a b)"), in_=bank(b),
                    func=AF.Gelu_apprx_tanh), [PS(b)], [("ug", sp_)])
            wdone(1)
            sl = wnext()
            for sp_ in range(8):
                b = gbank()
                for k in range(2):
                    sub = sp_ * 2 + k
                    for c in range(8):
                        S.op("pe", lambda e, c=c, k=k, sub=sub, b=b: e.matmul(
                            bank(b)[:, k * 256:(k + 1) * 256], lhsT=big[:, c, sub * 128:(sub + 1) * 128],
                            rhs=wr[sl][:, c, :], start=(c == 0), stop=(c == 7)),
                            [("big", c, sub), ("wr", sl)], [PS(b)])
                S.op("act", lambda e, b=b: e.activation(out=vg, in_=bank(b), func=AF.Gelu_apprx_tanh),
                     [PS(b)], ["vg"])
                vg3 = vg.rearrange("p (a d) -> p a d", d=64)
                cen3 = cen.rearrange("p (a d) -> p a d", d=64)
                sq3 = sqb.rearrange("p (a d) -> p a d", d=64)
                S.op("dve", lambda e: e.tensor_reduce(out=st1, in_=vg3, axis=AX.X, op=ALU.add), ["vg"], ["st1"])
                S.op("dve", lambda e: e.tensor_scalar(out=st2, in0=st1, scalar1=1.0 / 64, scalar2=None,
                                                      op0=ALU.mult), ["st1"], ["st2"])
                S.op("dve", lambda e: e.tensor_tensor(out=cen3, in0=vg3,
                                                      in1=st2.unsqueeze(2).to_broadcast([128, 8, 64]),
                                                      op=ALU.subtract), ["vg", "st2"], ["cen"])
                S.op("act", lambda e: e.activation(out=sqb, in_=cen, func=AF.Square), ["cen"], ["sqb"])
                S.op("dve", lambda e: e.tensor_reduce(out=st1, in_=sq3, axis=AX.X, op=ALU.add), ["sqb"], ["st1"])
                S.op("act", lambda e: e.activation(out=st3, in_=st1, func=AF.Sqrt, bias=EPS, scale=1.0 / 64),
                     ["st1"], ["st3"])
                S.op("dve", lambda e: e.reciprocal(out=st2, in_=st3), ["st3"], ["st2"])
                S.op("dve", lambda e: e.tensor_tensor(out=cen3, in0=cen3,
                                                      in1=st2.unsqueeze(2).to_broadcast([128, 8, 64]),
                                                      op=ALU.mult), ["cen", "st2"], ["cen"])
                cen2 = cen.rearrange("p (k n) -> p k n", k=2)
                S.op("dve", lambda e: e.tensor_tensor(out=cen2, in0=cen2,
                                                      in1=lng.unsqueeze(1).to_broadcast([128, 2, 256]),
                                                      op=ALU.mult), ["cen", "lng"], ["cen"])
                S.op("dve", lambda e: e.tensor_tensor(out=vnb.rearrange("p (k n) -> p k n", k=2), in0=cen2,
                                                      in1=lnb.unsqueeze(1).to_broadcast([128, 2, 256]),
                                                      op=ALU.add), ["cen", "lnb"], ["vnb"])
                b2 = gbank()
                for k in range(2):
                    for g in range(4):
                        S.op("pe", lambda e, k=k, g=g, b2=b2: e.matmul(
                            bank(b2)[:, k * 256 + g * 64:k * 256 + (g + 1) * 64], lhsT=WmT[:, g, :],
                            rhs=vnb[:, k * 256 + g * 64:k * 256 + (g + 1) * 64], start=True, stop=True),
                            ["vnb", "WmT"], [PS(b2)])
                for k in range(2):
                    sub = sp_ * 2 + k
                    for g in range(4):
                        S.op("dve", lambda e, k=k, g=g, sub=sub, b2=b2: e.scalar_tensor_tensor(
                            out=ab[:, k * 256 + g * 64:k * 256 + (g + 1) * 64],
                            in0=bank(b2)[:, k * 256 + g * 64:k * 256 + (g + 1) * 64], scalar=bs[:, g:g + 1],
                            in1=ug[:, sub, g * 64:(g + 1) * 64], op0=ALU.add, op1=ALU.mult),
                            [PS(b2), "bs", ("ug", sp_)], ["ab"])
                tb = 6 + (sp_ % 2)
                for k in range(2):
                    for hf in range(2):
                        i4 = k * 2 + hf
                        S.op("pe", lambda e, i4=i4, tb=tb: e.transpose(
                            out=bankT(tb)[:, i4 * 128:(i4 + 1) * 128], in_=ab[:, i4 * 128:(i4 + 1) * 128],
                            identity=identb), ["ab", "identb"], [PS(tb)])
                for k in range(2):
                    sub = sp_ * 2 + k
                    S.op("act", lambda e, k=k, sub=sub, tb=tb: e.activation(
                        out=aT[:, :, sub * 128:(sub + 1) * 128],
                        in_=bankT(tb)[:, k * 256:(k + 1) * 256].rearrange("p (a t) -> p a t", a=2),
                        func=AF.Copy), [PS(tb)], [("aT", sub)])
            wdone(1)
            ev = 0
            for u in range(2, 8):
                if u > 2:
                    wdone(1)
                sl = wnext()
                isq = u < 5
                dstT = qT if isq else kT
                for hf in range(2):
                    pair = ((u - 2) % 3) * 2 + hf
                    for tb_ in range(4):
                        b = gbank()
                        for c in range(8):
                            S.op("pe", lambda e, c=c, hf=hf, tb_=tb_, b=b: e.matmul(
                                bank(b), lhsT=wr[sl][:, c, hf * 128:(hf + 1) * 128],
                                rhs=big[:, c, tb_ * 512:(tb_ + 1) * 512], start=(c == 0), stop=(c == 7)),
                                [("wr", sl)] + [("big", c, 4 * tb_ + i) for i in range(4)], [PS(b)])
                        dst = dstT[:, pair, tb_ * 512:(tb_ + 1) * 512]
                        key = ("qT" if isq else "kT", pair, tb_)
                        sc = 0.125 if isq else 1.0
                        if not isq:
                            S.op("act", lambda e, dst=dst, b=b, sc=sc: e.activation(
                                out=dst, in_=bank(b), func=AF.Copy), [PS(b)], [key])
                        else:
                            S.op("dve", lambda e, dst=dst, b=b, sc=sc: e.tensor_scalar(
                                out=dst, in0=bank(b), scalar1=sc, scalar2=None, op0=ALU.mult), [PS(b)], [key])
                        ev += 1
            wdone(1)
            vcount = 0
            for u in range(8, 11):
                if u > 8:
                    wdone(1)
                sl = wnext()
                for sp_ in range(8):
                    b = gbank()
                    for k in range(2):
                        sub = sp_ * 2 + k
                        for c in range(8):
                            S.op("pe", lambda e, c=c, k=k, sub=sub, b=b: e.matmul(
                                bank(b)[:, k * 256:(k + 1) * 256], lhsT=big[:, c, sub * 128:(sub + 1) * 128],
                                rhs=wr[sl][:, c, :], start=(c == 0), stop=(c == 7)),
                                [("big", c, sub), ("wr", sl)], [PS(b)])
                    for hf in range(2):
                        pair = (u - 8) * 2 + hf
                        vslot = vcount % 2
                        vcount += 1
                        srcv = bank(b).rearrange("p (k h d) -> p k h d", k=2, h=4)[:, :, hf * 2:hf * 2 + 2, :]
                        if vcount % 2 == 0:
                            S.op("act", lambda e, vslot=vslot, srcv=srcv: e.activation(
                                out=vst[vslot][:, :, :, 0:64], in_=srcv, func=AF.Copy), [PS(b)], [("vst", vslot)])
                        else:
                            S.op("dve", lambda e, vslot=vslot, srcv=srcv: e.tensor_copy(
                                out=vst[vslot][:, :, :, 0:64], in_=srcv), [PS(b)], [("vst", vslot)])
                        S.dma("sp", vs[pair, sp_ * 256:(sp_ + 1) * 256, :].rearrange("(k p) e -> p k e", p=128),
                              vst[vslot].rearrange("p k h d -> p k (h d)"),
                              [("vst", vslot)], [("vs", pair, sp_)], ("vst", vslot))

            wdone(1)
            if dbg and s == 0:
                dump("d_qT", qT.rearrange("p a b -> p (a b)"), [128, 6 * T], [("qT", p_, i) for p_ in range(6) for i in range(4)])
                dump("d_kT", kT.rearrange("p a b -> p (a b)"), [128, 6 * T], [("kT", p_, i) for p_ in range(6) for i in range(4)])
                dump("d_aT", aT.rearrange("p a b -> p (a b)"), [128, 2 * T], [("aT", i) for i in range(16)])
                dump("d_ug", ug.rearrange("p a b -> p (a b)"), [128, 16 * 256], [("ug", i) for i in range(8)])
                dump("d_vs", vs.rearrange("a t e -> (a t) e"), [6 * T, 130], [("vs", p_, i) for p_ in range(6) for i in range(8)])
            if stop_stage < 3:
                break
            def load_V(pair):
                for c in range(3):
                    slot = (pair * 3 + c) % 6
                    if c == 0:
                        src = vs[pair].rearrange("(b p) e -> p b e", p=128)
                    elif c == 1:
                        src = vs[pair].rearrange("(n p r) e -> p n r e", n=4, p=128, r=4)
                    else:
                        src = vs[pair].rearrange("(p r) e -> p r e", r=16)
                    dst = Vc[slot] if c != 1 else Vc[slot].rearrange("p (n r) e -> p n r e", n=4)
                    S.dma("sp", dst, src, [("vs", pair, i) for i in range(8)], [("vc", slot)], ("vc", slot))

            qkeys = lambda pair: [("qT", pair, i) for i in range(4)]
            kkeys = lambda pair: [("kT", pair, i) for i in range(4)]
            groups = []
            for pair in range(6):
                for hl in range(2):
                    h = pair * 2 + hl
                    pr = slice(hl * 64, hl * 64 + 64)
                    tiles = []
                    for r in range(16):
                        tiles.append(dict(k=kT[pr, pair, r::16], q=qT[pr, pair, r::16], cp=4, c=2, blk=r,
                                          outs=[(pA[0:65, bb * 512 + r:(bb + 1) * 512:16], slice(32 * bb, 32 * bb + 32), bb)
                                                for bb in range(4)], fin=None))
                    for bb in range(4):
                        for r4 in range(4):
                            for pc in range(2):
                                if pc == 1 and bb == 0:
                                    continue
                                kb = bb - pc
                                tiles.append(dict(k=kT[pr, pair, kb * 512 + r4:(kb + 1) * 512:4],
                                                  q=qT[pr, pair, bb * 512 + r4:(bb + 1) * 512:4],
                                                  cp=2 + pc, c=1, blk=kb * 4 + r4,
                                                  outs=[(pA[0:65, bb * 512 + r4:(bb + 1) * 512:4], slice(0, 128), bb)],
                                                  fin=None))
                        for n in range(4 * bb, 4 * bb + 4):
                            for pc in range(2):
                                if pc == 1 and n == 0:
                                    continue
                                kb = n - pc
                                tiles.append(dict(k=kT[pr, pair, kb * 128:(kb + 1) * 128],
                                                  q=qT[pr, pair, n * 128:(n + 1) * 128],
                                                  cp=pc, c=0, blk=kb,
                                                  outs=[(pA[0:65, n * 128:(n + 1) * 128], slice(0, 128), bb)],
                                                  fin=None))
                        tiles[-1]["fin"] = bb
                    for i in range(0, len(tiles), 4):
                        groups.append(dict(pair=pair, hl=hl, h=h, tiles=tiles[i:i + 4], first=(i == 0)))

            gstate = {"n": 0, "started": None}

            def emit_qk(g, gi):
                stb = 4 + (gi % 2)
                pair, h = g["pair"], g["h"]
                for j, t in enumerate(g["tiles"]):
                    o = bank(stb)[:, j * 128:(j + 1) * 128]
                    S.op("pe", lambda e, o=o, t=t: e.matmul(o, lhsT=t["k"], rhs=t["q"], start=True, stop=False),
                         qkeys(pair) + kkeys(pair), [PS(stb)])
                    S.op("pe", lambda e, o=o, t=t, h=h: e.matmul(o, lhsT=biasT[:, t["cp"], h, :], rhs=Jb,
                                                                 start=False, stop=True),
                         ["biasT", "Jb"], [PS(stb)])

            def emit_exp_pv(g, gi):
                stb = 4 + (gi % 2)
                pslot = gi % 3
                n = len(g["tiles"])
                pair, hl = g["pair"], g["hl"]
                if g["first"]:
                    gstate["started"] = [False] * 4
                S.op("act", lambda e: e.activation(out=PT[pslot][:, 0:n * 128], in_=bank(stb)[:, 0:n * 128],
                                                   func=AF.Exp), [PS(stb)], [("pt", pslot)])
                posts = []
                for j, t in enumerate(g["tiles"]):
                    vslot = (pair * 3 + t["c"]) % 6
                    lhsT = Vc[vslot][:, t["blk"], hl * 65:(hl + 1) * 65]
                    for (oap, csl, bb) in t["outs"]:
                        stf = not gstate["started"][bb]
                        gstate["started"][bb] = True
                        rhs = PT[pslot][:, j * 128 + csl.start:j * 128 + csl.stop]
                        S.op("pe", lambda e, oap=oap, lhsT=lhsT, rhs=rhs, stf=stf: e.matmul(
                            oap, lhsT=lhsT, rhs=rhs, start=stf, stop=False, skip_group_check=True),
                            [("pt", pslot), ("vc", vslot)], [PS(bb)])
                    if t["fin"] is not None:
                        posts.append(t["fin"])
                return posts

            def emit_post1(g, bb):
                S.op("dve", lambda e: e.reciprocal(out=rden[64:65, :], in_=pA[64:65, bb * 512:(bb + 1) * 512]),
                     [PS(bb)], ["rden"])

            def emit_post2(g, bb):
                pair, hl = g["pair"], g["hl"]
                S.op("pe", lambda e: e.matmul(bank(6)[0:64, :], lhsT=onesf[64:65, 0:64], rhs=rden[64:65, :],
                                              start=True, stop=True), ["rden", "onesf"], [PS(6)])
                S.op("act", lambda e: e.activation(out=bcs[0:64, :], in_=bank(6)[0:64, :], func=AF.Copy),
                     [PS(6)], ["bcs"])
                if hl == 0:
                    dst = big[0:64, 2 + pair, bb * 512:(bb + 1) * 512]
                    wk = [("big", 2 + pair, 4 * bb + i) for i in range(4)]
                else:
                    dst = ostg[0:64, bb * 512:(bb + 1) * 512]
                    wk = ["ostg"]
                S.op("dve", lambda e: e.tensor_tensor(out=dst, in0=pA[0:64, bb * 512:(bb + 1) * 512],
                                                      in1=bcs[0:64, :], op=ALU.mult),
                     [PS(bb), "bcs"], wk)
                if hl == 1 and bb == 3:
                    S.dma("sp", big[64:128, 2 + pair, :], ostg[0:64, :], ["ostg"],
                          [("big", 2 + pair, i) for i in range(16)], "ostg")

            load_V(0)
            if dbg and s == 0:
                for c_ in range(3):
                    dump("d_vc%d" % c_, Vc[c_].rearrange("p a b -> p (a b)"), [128, 16 * 130], [("vc", c_)])
            deferred = []
            loaded = {0}
            for gi, g in enumerate(groups):
                if g["first"] and g["hl"] == 1 and g["pair"] + 1 < 6 and (g["pair"] + 1) not in loaded:
                    load_V(g["pair"] + 1)
                    loaded.add(g["pair"] + 1)
                emit_qk(g, gi)
                for (pg, bb) in deferred:
                    emit_post2(pg, bb)
                deferred = []
                if gi > 0:
                    pg = groups[gi - 1]
                    for bb in emit_exp_pv(pg, gi - 1):
                        emit_post1(pg, bb)
                        deferred.append((pg, bb))
                        if len(deferred) > 1:
                            emit_post2(*deferred.pop(0))
            pg = groups[-1]
            for (dg, bb) in deferred:
                emit_post2(dg, bb)
            deferred = []
            for bb in emit_exp_pv(pg, len(groups) - 1):
                emit_post1(pg, bb)
                emit_post2(pg, bb)

            if dbg and s == 0:
                dump("d_mixT", big.rearrange("p a b -> p (a b)"), [128, 8 * T], [("big", c, i) for c in range(8) for i in range(16)])
            if stop_stage < 4:
                break
            S.barrier()
            AR.release(m_seq)
            m_b = AR.mark()
            tmpb = [AR.alloc([D], F32) for _ in range(2)]
            x1b = [AR.alloc([D], F32) for _ in range(2)]
            wsl = [wnext() for _ in range(4)]
            for sub in range(16):
                ys = sub % 2
                for half in range(2):
                    b = ys * 2 + half
                    for u2 in range(2):
                        unit = half * 2 + u2
                        for c in range(8):
                            lhsT = (aT if c < 2 else big)[:, c, sub * 128:(sub + 1) * 128]
                            rk = ("aT", sub) if c < 2 else ("big", c, sub)
                            S.op("pe", lambda e, b=b, u2=u2, unit=unit, c=c, lhsT=lhsT: e.matmul(
                                bank(b)[:, u2 * 256:(u2 + 1) * 256], lhsT=lhsT, rhs=wr[wsl[unit]][:, c, :],
                                start=(c == 0), stop=(c == 7)), [rk, ("wr", wsl[unit])], [PS(b)])
                rows = slice(base + sub * 128, base + (sub + 1) * 128)
                post_norm_residual(pA[:, ys * 1024:(ys + 1) * 1024], [ys * 2, ys * 2 + 1], 1,
                                   x[rows, :], out[rows, :], tmpb, x1b, sub)
            wdone(4)
            S.barrier()
            AR.release(m_b)
            if dbg:
                break

        if not stop_after_mixer:
            AR.release(m_setup)
            load_gains(2)
            wd = AR.alloc([NFC, D], BF16)
            h2T = AR.alloc([8, 1024], BF16)
            hidT = AR.alloc([NFC, 1024], BF16)
            graw = [AR.alloc([1026], F32) for _ in range(2)]
            cbuf = [AR.alloc([1024], F32) for _ in range(2)]
            gl = [AR.alloc([1024], BF16) for _ in range(2)]
            halo = AR.alloc([NFC, 2], F32)
            tmpb = [AR.alloc([D], F32) for _ in range(2)]
            x1b = [AR.alloc([D], F32) for _ in range(2)]
            w_down_v = w_down.rearrange("(j p) n -> p j n", p=128)
            fcn = 0
            if dbgf:
                dump("d_x1", out[0:1024, :], [1024, D], [], F32)
            for tt in range(4):
                tbase = tt * 1024
                for sub in range(8):
                    norm_transpose(out[tbase + sub * 128: tbase + (sub + 1) * 128, :], 0, sub,
                                   h2T[:, :, sub * 128:(sub + 1) * 128], [("h2T", c, sub) for c in range(8)])
                for j in range(11):
                    slg = wnext()
                    slu = wnext()
                    if tt == 0:
                        S.dma("pool", wd[:, 2 * j:2 * j + 2, :], w_down_v[:, 2 * j:2 * j + 2, :], [],
                              [("wd", j)], ("wd", j))
                    for hf in range(2):
                        fc = 2 * j + hf
                        pb = 0 if fcn % 2 == 0 else 4
                        fcn += 1
                        P = pA if pb == 0 else pB
                        for (wsl_, boff) in ((slg, 0), (slu, 2)):
                            for tb_ in range(2):
                                b = pb + boff + tb_
                                for c in range(8):
                                    S.op("pe", lambda e, b=b, c=c, wsl_=wsl_, hf=hf, tb_=tb_: e.matmul(
                                        bank(b), lhsT=wr[wsl_][:, c, hf * 128:(hf + 1) * 128],
                                        rhs=h2T[:, c, tb_ * 512:(tb_ + 1) * 512], start=(c == 0), stop=(c == 7)),
                                        [("wr", wsl_)] + [("h2T", c, 4 * tb_ + i) for i in range(4)], [PS(b)])
                        gps = P[:, 0:1024]
                        ups = P[:, 1024:2048]
                        gk = [PS(pb), PS(pb + 1)]
                        uk = [PS(pb + 2), PS(pb + 3)]
                        rs_ = fcn % 2
                        gr = graw[rs_]
                        cb_ = cbuf[rs_]
                        if tt % 2 == 0:
                            S.op("dve", lambda e, gr=gr: e.memset(gr[:, 0:2], 0.0), [], [("graw", rs_)])
                        else:
                            S.op("dve", lambda e, gr=gr, fc=fc: e.tensor_copy(out=gr[:, 0:2], in_=halo[:, fc, :]),
                                 [("halo", fc)], [("graw", rs_)])
                        S.op("act", lambda e, gr=gr, gps=gps: e.activation(out=gr[:, 2:1026], in_=gps, func=AF.Copy),
                             gk, [("graw", rs_)])
                        S.op("act", lambda e, cb_=cb_, gps=gps, fc=fc: e.activation(
                            out=cb_, in_=gps, func=AF.Identity, bias=cwt[:, 3, fc:fc + 1],
                            scale=cwt[:, 2, fc:fc + 1]), gk + ["cwt"], [("cbuf", rs_)])
                        S.op("dve", lambda e, gr=gr, fc=fc: e.tensor_copy(out=halo[:, fc, :], in_=gr[:, 1024:1026]),
                             [("graw", rs_)], [("halo", fc)])
                        S.op("dve", lambda e, gr=gr, cb_=cb_, fc=fc: e.scalar_tensor_tensor(
                            out=cb_, in0=gr[:, 1:1025], scalar=cwt[:, 1, fc:fc + 1], in1=cb_,
                            op0=ALU.mult, op1=ALU.add), [("graw", rs_), ("cbuf", rs_), "cwt"], [("cbuf", rs_)])
                        S.op("dve", lambda e, gr=gr, cb_=cb_, fc=fc: e.scalar_tensor_tensor(
                            out=cb_, in0=gr[:, 0:1024], scalar=cwt[:, 0, fc:fc + 1], in1=cb_,
                            op0=ALU.mult, op1=ALU.add), [("graw", rs_), ("cbuf", rs_), "cwt"], [("cbuf", rs_)])
                        S.op("act", lambda e, cb_=cb_, rs_=rs_: e.activation(out=gl[rs_], in_=cb_,
                                                                             func=AF.Gelu_apprx_tanh),
                             [("cbuf", rs_)], [("gl", rs_)])
                        S.op("dve", lambda e, ups=ups, rs_=rs_, fc=fc: e.tensor_tensor(
                            out=hidT[:, fc, :], in0=ups, in1=gl[rs_], op=ALU.mult),
                            uk + [("gl", rs_)], [("hidT", fc)])
                    wdone(2)
                if dbgf and tt == 0:
                    dump("d_h2T", h2T.rearrange("p a b -> p (a b)"), [128, 8 * 1024], [("h2T", c, i) for c in range(8) for i in range(8)])
                    dump("d_hidT", hidT.rearrange("p a b -> p (a b)"), [128, NFC * 1024], [("hidT", i) for i in range(NFC)])
                for sub in range(8):
                    ys = sub % 2
                    for half in range(2):
                        b = ys * 2 + half
                        for fc in range(NFC):
                            S.op("pe", lambda e, b=b, fc=fc, half=half, sub=sub: e.matmul(
                                bank(b), lhsT=hidT[:, fc, sub * 128:(sub + 1) * 128],
                                rhs=wd[:, fc, half * 512:(half + 1) * 512], start=(fc == 0), stop=(fc == NFC - 1)),
                                [("hidT", fc), ("wd", fc // 2)], [PS(b)])
                    rows = slice(tbase + sub * 128, tbase + (sub + 1) * 128)
                    post_norm_residual(pA[:, ys * 1024:(ys + 1) * 1024], [ys * 2, ys * 2 + 1], 1,
                                       out[rows, :], out[rows, :], tmpb, x1b, tt * 8 + sub)
        S.finish()
        nsem = S.emit()
    return nc


_CACHE = {}


def _run(inputs, stop_after_mixer=False, stop_stage=99, dbg=False, dbgf=False):
    ident, J, tril, oh = _host_consts()
    f = lambda a: np.ascontiguousarray(np.asarray(a, dtype=np.float32))
    x = f(inputs["x"]).reshape(16 * T, D)
    common = {
        "g0": f(inputs["norm_mix_pre"]).reshape(1, D),
        "g1": f(inputs["norm_mix_post"]).reshape(1, D),
        "g2": f(inputs["norm_ffn_pre"]).reshape(1, D),
        "g3": f(inputs["norm_ffn_post"]).reshape(1, D),
        "w_in": f(inputs["w_in"]).reshape(D, INC),
        "ln_g": f(inputs["ln_v_gain"]).reshape(1, 256),
        "ln_b": f(inputs["ln_v_bias"]).reshape(1, 256),
        "sp_w": f(inputs["spatial_w"]).reshape(4, 128, 128),
        "sp_b": f(inputs["spatial_b"]).reshape(4, 128),
        "rel_b": f(inputs["rel_bias"]).reshape(32, 12),
        "w_out": f(inputs["w_out"]).reshape(D, D),
        "w_gate": f(inputs["w_gate"]).reshape(D, DFF),
        "w_up": f(inputs["w_up"]).reshape(D, DFF),
        "conv_w": f(inputs["conv_w"]).reshape(3, DFF),
        "conv_b": f(inputs["conv_b"]).reshape(1, DFF),
        "w_down": f(inputs["w_down"]).reshape(DFF, D),
        "c_ident": ident, "c_J": J, "c_tril": tril, "c_oh": oh,
    }
    key = (bool(stop_after_mixer), stop_stage, dbg, dbgf)
    if key not in _CACHE:
        _CACHE[key] = build_nc(stop_after_mixer, stop_stage, dbg, dbgf)
    nc = _CACHE[key]
    in_maps = []
    for c in range(8):
        m = dict(common)
        m["x"] = np.ascontiguousarray(x[c * NTOK:(c + 1) * NTOK])
        in_maps.append(m)
    res = run_bass_kernel_spmd(nc, in_maps, core_ids=list(range(8)))
    if dbg or dbgf:
        return res.results[0]
    outs = [np.asarray(r["out"], dtype=np.float32) for r in res.results]
    return np.concatenate(outs, axis=0).reshape(16, T, D)


def kernel(**inputs):
    return _run(inputs)
```
